# Optimizing a Trainium2 kernel written in Bass

```python
import jax
import jax.numpy as jnp
from jax import lax
import numpy as np

D_MODEL = 1024
BATCH = 8
SEQ = 4096
DEPTH = 4

GRID_W = 64
CTX_LEN = 256

N_MIXERS = 3
MIX_POOL, MIX_MLA, MIX_NA = 0, 1, 2
N_POOL_LAYERS = (DEPTH + 2) // 3
N_MLA_LAYERS = (DEPTH + 1) // 3
N_NA_LAYERS = DEPTH // 3

RMS_EPS = 1e-6

POOL_WINDOWS = (2, 4, 8, 16)
POOL_GROUPS = len(POOL_WINDOWS)
POOL_GROUP_DIM = D_MODEL // POOL_GROUPS

MLA_HEADS = D_MODEL // 128
MLA_NOPE = 128
MLA_ROPE = 64
MLA_V = 128
MLA_Q_LORA = 384
MLA_KV_LORA = 256
MLA_IN = MLA_Q_LORA + MLA_KV_LORA + MLA_ROPE
MLA_SCALE = (MLA_NOPE + MLA_ROPE) ** -0.5
ROPE_BASE = 10000.0
Q_BLOCK = 128

NA_HEADS = D_MODEL // 64
NA_HEAD_DIM = 64
NA_WIDTH = NA_HEADS * NA_HEAD_DIM
NA_ROWS = 8
NA_COLS = 16
NA_SCALE = NA_HEAD_DIM ** -0.5

FFN_HIDDEN = -(-8 * D_MODEL // (3 * 256)) * 256

kernel_name = "hybrid_pool_mla_natten_prefix_dit"


def rmsnorm(x, g):
    xf = x.astype(jnp.float32)
    y = xf * lax.rsqrt(jnp.mean(xf * xf, axis=-1, keepdims=True) + RMS_EPS)
    return (y * g.astype(jnp.float32)).astype(x.dtype)


def modulate(h, shift, scale):
    return h * (1 + scale) + shift


def adaln(cond, w, b):
    m = jax.nn.silu(cond) @ w + b
    return jnp.split(m, 6, axis=-1)


def swiglu(h, w_gu, w_down):
    g, u = jnp.split(h @ w_gu, 2, axis=-1)
    return (jax.nn.silu(g) * u) @ w_down


def attend(q, k, v, scale):
    s = jnp.einsum('bqhd,bkhd->bhqk', q, k).astype(jnp.float32) * scale
    p = jax.nn.softmax(s, axis=-1).astype(v.dtype)
    return jnp.einsum('bhqk,bkhd->bqhd', p, v)


def blocked_attend(q, k, v, scale):
    B, L, H, dk = q.shape
    nb = L // Q_BLOCK
    qb = q.reshape(B, nb, Q_BLOCK, H, dk).transpose(1, 0, 2, 3, 4)
    out = lax.map(lambda qi: attend(qi, k, v, scale), qb)
    return out.transpose(1, 0, 2, 3, 4).reshape(B, L, H, v.shape[-1])


def axial_rope_tables(L):
    t = jnp.arange(L)
    row = (t // GRID_W).astype(jnp.float32)
    col = (t % GRID_W).astype(jnp.float32)
    half = MLA_ROPE // 2
    inv = ROPE_BASE ** (-jnp.arange(0, half, 2, dtype=jnp.float32) / half)
    ang = jnp.concatenate([row[:, None] * inv, col[:, None] * inv], axis=-1)
    return jnp.cos(ang), jnp.sin(ang)


def apply_rope(x, cos, sin):
    half = x.shape[-1] // 2
    x1 = x[..., :half].astype(jnp.float32)
    x2 = x[..., half:].astype(jnp.float32)
    return jnp.concatenate([x1 * cos - x2 * sin, x1 * sin + x2 * cos], axis=-1).astype(x.dtype)


def pool_mixer(h, w, scale):
    B, L, D = h.shape
    hf = h.astype(jnp.float32)
    cs = jnp.concatenate([jnp.zeros((B, 1, D), jnp.float32), jnp.cumsum(hf, axis=1)], axis=1)
    pos = jnp.arange(L)
    groups = []
    for g, win in enumerate(POOL_WINDOWS):
        lo = jnp.clip(pos - win // 2, 0, L)
        hi = jnp.clip(pos + win - win // 2, 0, L)
        sl = slice(g * POOL_GROUP_DIM, (g + 1) * POOL_GROUP_DIM)
        cnt = (hi - lo).astype(jnp.float32)[None, :, None]
        groups.append((cs[:, hi, sl] - cs[:, lo, sl]) / cnt - hf[:, :, sl])
    pooled = jnp.stack(groups, axis=2).astype(h.dtype)
    y = jnp.einsum('blgc,gcd->blgd', pooled, w).reshape(B, L, D)
    return y * scale


def mla_queries(q_a, q_norm_g, w_qb):
    B, L, _ = q_a.shape
    q = (rmsnorm(q_a, q_norm_g) @ w_qb).reshape(B, L, MLA_HEADS, MLA_NOPE + MLA_ROPE)
    return q[..., :MLA_NOPE], q[..., MLA_NOPE:]


def mla_keys_values(kv_a, k_rope, kv_norm_g, w_kvb):
    B, L, _ = kv_a.shape
    kv = (rmsnorm(kv_a, kv_norm_g) @ w_kvb).reshape(B, L, MLA_HEADS, MLA_NOPE + MLA_V)
    k_rope_h = jnp.broadcast_to(k_rope[:, :, None, :], (B, L, MLA_HEADS, MLA_ROPE))
    k = jnp.concatenate([kv[..., :MLA_NOPE], k_rope_h], axis=-1)
    return k, kv[..., MLA_NOPE:]


def mla_mixer(h_lat, h_ctx, w_in, q_norm_g, kv_norm_g, w_qb, w_kvb, w_o, need_ctx_out):
    B, L, _ = h_lat.shape
    Lc = h_ctx.shape[1]
    cos, sin = axial_rope_tables(L)
    a_lat = h_lat @ w_in
    qn, qr = mla_queries(a_lat[..., :MLA_Q_LORA], q_norm_g, w_qb)
    q_lat = jnp.concatenate([qn, apply_rope(qr, cos[:, None, :], sin[:, None, :])], axis=-1)
    k_lat, v_lat = mla_keys_values(a_lat[..., MLA_Q_LORA:MLA_Q_LORA + MLA_KV_LORA],
                                   apply_rope(a_lat[..., MLA_Q_LORA + MLA_KV_LORA:], cos, sin),
                                   kv_norm_g, w_kvb)
    a_ctx_kv = h_ctx @ w_in[:, MLA_Q_LORA:]
    k_ctx, v_ctx = mla_keys_values(a_ctx_kv[..., :MLA_KV_LORA], a_ctx_kv[..., MLA_KV_LORA:], kv_norm_g, w_kvb)
    k_all = jnp.concatenate([k_ctx, k_lat], axis=1)
    v_all = jnp.concatenate([v_ctx, v_lat], axis=1)
    y_lat = blocked_attend(q_lat, k_all, v_all, MLA_SCALE).reshape(B, L, MLA_HEADS * MLA_V) @ w_o
    y_ctx = None
    if need_ctx_out:
        qn_c, qr_c = mla_queries(h_ctx @ w_in[:, :MLA_Q_LORA], q_norm_g, w_qb)
        q_ctx = jnp.concatenate([qn_c, qr_c], axis=-1)
        y_ctx = attend(q_ctx, k_ctx, v_ctx, MLA_SCALE).reshape(B, Lc, MLA_HEADS * MLA_V) @ w_o
    return y_lat, y_ctx


def na_mixer(h_lat, h_ctx, w_in, rpb, w_o, need_ctx_out):
    B, L, _ = h_lat.shape
    Lc = h_ctx.shape[1]
    rows = L // GRID_W
    kr = min(NA_ROWS, rows)
    qkv = (h_lat @ w_in).reshape(B, rows, GRID_W, 3, NA_HEADS, NA_HEAD_DIM)
    q_grid, k_grid, v_grid = qkv[:, :, :, 0], qkv[:, :, :, 1], qkv[:, :, :, 2]
    kv_ctx = (h_ctx @ w_in[:, NA_WIDTH:]).reshape(B, Lc, 2, NA_HEADS, NA_HEAD_DIM)
    k_ctx, v_ctx = kv_ctx[:, :, 0], kv_ctx[:, :, 1]

    cols = jnp.arange(GRID_W)
    col_start = jnp.clip(cols - NA_COLS // 2, 0, GRID_W - NA_COLS)
    col_mask = (cols[None, :] >= col_start[:, None]) & (cols[None, :] < col_start[:, None] + NA_COLS)
    dc_idx = jnp.clip(cols[None, :] - cols[:, None] + NA_COLS - 1, 0, 2 * NA_COLS - 2)
    rpb_f = rpb.astype(jnp.float32)
    row_ids = jnp.arange(rows)
    row_start = jnp.clip(row_ids - kr // 2, 0, rows - kr)

    def row_block(args):
        r, rs, q_r = args
        k_r = lax.dynamic_slice_in_dim(k_grid, rs, kr, axis=1)
        v_r = lax.dynamic_slice_in_dim(v_grid, rs, kr, axis=1)
        dr_idx = rs + jnp.arange(kr) - r + NA_ROWS - 1
        bias = rpb_f[:, dr_idx][:, :, dc_idx].transpose(0, 2, 1, 3)
        s_loc = jnp.einsum('bqhd,bikhd->bhqik', q_r, k_r).astype(jnp.float32) * NA_SCALE + bias
        s_loc = jnp.where(col_mask[:, None, :], s_loc, -jnp.inf)
        s_ctx = jnp.einsum('bqhd,bkhd->bhqk', q_r, k_ctx).astype(jnp.float32) * NA_SCALE
        s = jnp.concatenate([s_loc.reshape(B, NA_HEADS, GRID_W, kr * GRID_W), s_ctx], axis=-1)
        p = jax.nn.softmax(s, axis=-1).astype(v_r.dtype)
        p_loc = p[..., :kr * GRID_W].reshape(B, NA_HEADS, GRID_W, kr, GRID_W)
        p_ctx = p[..., kr * GRID_W:]
        return (jnp.einsum('bhqik,bikhd->bqhd', p_loc, v_r)
                + jnp.einsum('bhqk,bkhd->bqhd', p_ctx, v_ctx))

    o = lax.map(row_block, (row_ids, row_start, q_grid.transpose(1, 0, 2, 3, 4)))
    y_lat = o.transpose(1, 0, 2, 3, 4).reshape(B, L, NA_WIDTH) @ w_o
    y_ctx = None
    if need_ctx_out:
        q_ctx = (h_ctx @ w_in[:, :NA_WIDTH]).reshape(B, Lc, NA_HEADS, NA_HEAD_DIM)
        y_ctx = attend(q_ctx, k_ctx, v_ctx, NA_SCALE).reshape(B, Lc, NA_WIDTH) @ w_o
    return y_lat, y_ctx


def setup_inputs(seed: int = 0) -> dict:
    key = jax.random.key(seed)
    ks = jax.random.split(key, 20)
    D = D_MODEL
    G = POOL_GROUP_DIM

    def nrm(k, shape, scale):
        return jax.random.normal(k, shape, jnp.float32) * scale

    return {
        "x": nrm(ks[0], (BATCH, SEQ, D), 1.0),
        "c": nrm(ks[1], (BATCH, D), 1.0),
        "ctx": nrm(ks[2], (BATCH, CTX_LEN, D), 1.0),
        "c_ctx": nrm(ks[3], (D,), 1.0),
        "ada_w": nrm(ks[4], (DEPTH, D, 6 * D), 0.5 * D ** -0.5),
        "ada_b": nrm(ks[5], (DEPTH, 6 * D), 0.02),
        "norm_g": 1.0 + nrm(ks[6], (DEPTH, 4, D), 0.05),
        "ffn_w_gu": nrm(ks[7], (DEPTH, D, 2 * FFN_HIDDEN), D ** -0.5),
        "ffn_w_down": nrm(ks[8], (DEPTH, FFN_HIDDEN, D), FFN_HIDDEN ** -0.5),
        "pool_w": nrm(ks[9], (N_POOL_LAYERS, POOL_GROUPS, G, G), G ** -0.5),
        "pool_scale": 1.0 + nrm(ks[10], (N_POOL_LAYERS, D), 0.1),
        "mla_w_in": nrm(ks[11], (N_MLA_LAYERS, D, MLA_IN), D ** -0.5),
        "mla_q_norm": 1.0 + nrm(ks[12], (N_MLA_LAYERS, MLA_Q_LORA), 0.05),
        "mla_kv_norm": 1.0 + nrm(ks[13], (N_MLA_LAYERS, MLA_KV_LORA), 0.05),
        "mla_w_qb": nrm(ks[14], (N_MLA_LAYERS, MLA_Q_LORA, MLA_HEADS * (MLA_NOPE + MLA_ROPE)), MLA_Q_LORA ** -0.5),
        "mla_w_kvb": nrm(ks[15], (N_MLA_LAYERS, MLA_KV_LORA, MLA_HEADS * (MLA_NOPE + MLA_V)), MLA_KV_LORA ** -0.5),
        "mla_w_o": nrm(ks[16], (N_MLA_LAYERS, MLA_HEADS * MLA_V, D), (MLA_HEADS * MLA_V) ** -0.5),
        "na_w_in": nrm(ks[17], (N_NA_LAYERS, D, 3 * NA_WIDTH), D ** -0.5),
        "na_rpb": nrm(ks[18], (N_NA_LAYERS, NA_HEADS, 2 * NA_ROWS - 1, 2 * NA_COLS - 1), 0.5),
        "na_w_o": nrm(ks[19], (N_NA_LAYERS, NA_WIDTH, D), NA_WIDTH ** -0.5),
    }


def reference(x, c, ctx, c_ctx, ada_w, ada_b, norm_g, ffn_w_gu, ffn_w_down,
              pool_w, pool_scale, mla_w_in, mla_q_norm, mla_kv_norm, mla_w_qb, mla_w_kvb, mla_w_o,
              na_w_in, na_rpb, na_w_o):
    for i in range(DEPTH):
        kind = i % N_MIXERS
        j = i // N_MIXERS
        ctx_after = any(l % N_MIXERS != MIX_POOL for l in range(i + 1, DEPTH))
        ctx_here = ctx_after or kind != MIX_POOL

        sh_m, sc_m, g_m, sh_f, sc_f, g_f = adaln(c[:, None, :], ada_w[i], ada_b[i])
        h_lat = modulate(rmsnorm(x, norm_g[i, 0]), sh_m, sc_m)
        h_ctx = None
        if ctx_here:
            csh_m, csc_m, cg_m, csh_f, csc_f, cg_f = adaln(c_ctx, ada_w[i], ada_b[i])
            h_ctx = modulate(rmsnorm(ctx, norm_g[i, 0]), csh_m, csc_m)

        if kind == MIX_POOL:
            y_lat = pool_mixer(h_lat, pool_w[j], pool_scale[j])
            y_ctx = pool_mixer(h_ctx, pool_w[j], pool_scale[j]) if ctx_after else None
        elif kind == MIX_MLA:
            y_lat, y_ctx = mla_mixer(h_lat, h_ctx, mla_w_in[j], mla_q_norm[j], mla_kv_norm[j],
                                     mla_w_qb[j], mla_w_kvb[j], mla_w_o[j], ctx_after)
        else:
            y_lat, y_ctx = na_mixer(h_lat, h_ctx, na_w_in[j], na_rpb[j], na_w_o[j], ctx_after)

        x = x + g_m * rmsnorm(y_lat, norm_g[i, 1])
        f_lat = swiglu(modulate(rmsnorm(x, norm_g[i, 2]), sh_f, sc_f), ffn_w_gu[i], ffn_w_down[i])
        x = x + g_f * rmsnorm(f_lat, norm_g[i, 3])

        if ctx_after:
            ctx = ctx + cg_m * rmsnorm(y_ctx, norm_g[i, 1])
            f_ctx = swiglu(modulate(rmsnorm(ctx, norm_g[i, 2]), csh_f, csc_f), ffn_w_gu[i], ffn_w_down[i])
            ctx = ctx + cg_f * rmsnorm(f_ctx, norm_g[i, 3])
    return x
```

```python
import numpy as np
from contextlib import ExitStack
import concourse.bass as bass
import concourse.mybir as mybir
from concourse.bass_utils import run_bass_kernel_spmd

F32 = mybir.dt.float32
BF16 = mybir.dt.bfloat16
AF = mybir.ActivationFunctionType
ALU = mybir.AluOpType

D = 1024
L = 4096
LC = 256
T = L + LC
NCH = 8
HID = 2816
NJ = 22
DEPTH = 4
EPS = 1e-6
MLA_SCALE = 192.0 ** -0.5
NA_SCALE = 64.0 ** -0.5
NEG = -30000.0
GRID = 64

ENGS = ("pe", "act", "dve", "pool", "sp")


class Buf:
    __slots__ = ("w", "r")

    def __init__(self):
        self.w = None
        self.r = {}


class Prog:
    K_DMA = 24

    def __init__(self, nc, stack):
        self.nc = nc
        self.q = {e: [] for e in ENGS}
        self.sem = {e: stack.enter_context(nc.semaphore("s_" + e)) for e in ENGS if e != "sp"}
        self.dsem = [stack.enter_context(nc.semaphore("d%d" % i)) for i in range(self.K_DMA)]
        self.cnt = {e: 0 for e in ENGS}
        self.waited = {e: {} for e in ENGS}
        self.ndma = 0
        self.pe_open = False

    def semh(self, k):
        return self.dsem[k[1]] if isinstance(k, tuple) else self.sem[k]

    def _waits(self, eng, reads, writes, extra=()):
        deps = list(extra)
        for b in reads:
            if b.w is not None:
                deps.append(b.w)
        for b in writes:
            if b.w is not None:
                deps.append(b.w)
            deps.extend(b.r.items())
        need = {}
        for k, v in deps:
            if eng == "pe" and k == "pe":
                continue
            if self.waited[eng].get(k, 0) < v and need.get(k, 0) < v:
                need[k] = v
        for k, v in need.items():
            self.waited[eng][k] = v
        return list(need.items())

    def _stamp(self, me, reads, writes):
        k, v = me
        for b in reads:
            if b.r.get(k, 0) < v:
                b.r[k] = v
        for b in writes:
            b.w = me
            b.r = {}

    def op(self, eng, fn, reads=(), writes=(), inc=True):
        waits = self._waits(eng, reads, writes)
        if inc:
            self.cnt[eng] += 1
            me = (eng, self.cnt[eng])
            self.q[eng].append((waits, fn, (eng, 1)))
            if eng == "pe":
                self.pe_open = False
        else:
            assert eng == "pe"
            me = (eng, self.cnt[eng] + 1)
            self.q[eng].append((waits, fn, None))
            self.pe_open = True
        self._stamp(me, reads, writes)

    def dma(self, out, in_, reads=(), writes=(), q="sp"):
        n = self.ndma
        self.ndma += 1
        k = ("d", n % self.K_DMA)
        v = 16 * (n // self.K_DMA + 1)
        extra = [(k, v - 16)] if v > 16 else []
        waits = self._waits(q, reads, writes, extra)
        self.q[q].append((waits, lambda e: e.dma_start(out=out, in_=in_), (k, 16)))
        self._stamp((k, v), reads, writes)

    def barrier(self):
        assert not self.pe_open
        allk = [(e, self.cnt[e]) for e in ENGS if e != "sp" and self.cnt[e] > 0]
        for i in range(self.K_DMA):
            n_i = (self.ndma - 1 - i) // self.K_DMA + 1 if self.ndma > i else 0
            if n_i > 0:
                allk.append((("d", i), 16 * n_i))
        for e in ENGS:
            need = []
            for k, v in allk:
                if self.waited[e].get(k, 0) < v:
                    self.waited[e][k] = v
                    need.append((k, v))
            if need:
                self.q[e].append((need, None, None))

    def emit(self):
        assert not self.pe_open
        self.barrier()
        nc = self.nc

        def run(name):
            def f(e):
                for waits, fn, inc in self.q[name]:
                    for k, v in waits:
                        e.wait_ge(self.semh(k), v)
                    if fn is not None:
                        ins = fn(e)
                        if inc is not None:
                            ins.then_inc(self.semh(inc[0]), inc[1])
            return f

        with nc.Block() as block:
            block.tensor(run("pe"))
            block.scalar(run("act"))
            block.vector(run("dve"))
            block.gpsimd(run("pool"))
            block.sync(run("sp"))


def mm(P, psb, out, lhsT, rhs, reads, start, stop, inc=None):
    P.op("pe", lambda e: e.matmul(out, lhsT=lhsT, rhs=rhs, start=start, stop=stop),
         reads=reads, writes=[psb], inc=(stop if inc is None else (inc or stop)))


def tt(P, eng, out, in0, in1, op, reads, writes):
    P.op(eng, lambda e: e.tensor_tensor(out=out, in0=in0, in1=in1, op=op), reads, writes)


def ts(P, eng, out, in0, s1, s2, op0, op1, reads, writes):
    if op1 is None:
        P.op(eng, lambda e: e.tensor_scalar(out=out, in0=in0, scalar1=s1, scalar2=None, op0=op0), reads, writes)
    else:
        P.op(eng, lambda e: e.tensor_scalar(out=out, in0=in0, scalar1=s1, scalar2=s2, op0=op0, op1=op1), reads, writes)


def stt(P, eng, out, in0, scalar, in1, op0, op1, reads, writes):
    P.op(eng, lambda e: e.scalar_tensor_tensor(out=out, in0=in0, scalar=scalar, in1=in1, op0=op0, op1=op1),
         reads, writes)


def actf(P, out, in_, func, reads, writes, scale=1.0, bias=None):
    if bias is None:
        P.op("act", lambda e: e.activation(out=out, in_=in_, func=func, scale=scale), reads, writes)
    else:
        P.op("act", lambda e: e.activation(out=out, in_=in_, func=func, scale=scale, bias=bias), reads, writes)


def cpy(P, eng, out, in_, reads, writes):
    if eng == "act":
        P.op("act", lambda e: e.activation(out=out, in_=in_, func=AF.Identity), reads, writes)
    else:
        P.op(eng, lambda e: e.tensor_copy(out=out, in_=in_), reads, writes)


def mset(P, eng, ap, val, writes):
    P.op(eng, lambda e: e.memset(ap, val), (), writes)


class Ring:
    def __init__(self, P, slots, total, loader):
        self.P, self.slots, self.total, self.loader = P, slots, total, loader
        self.issued = 0

    def prime(self):
        while self.issued < min(len(self.slots), self.total):
            self._load()

    def _load(self):
        n = self.issued
        ap, buf = self.slots[n % len(self.slots)]
        self.loader(n, ap, buf)
        self.issued += 1

    def get(self, n):
        assert n < self.issued
        return self.slots[n % len(self.slots)]

    def done(self, n):
        if self.issued < self.total and self.issued <= n + len(self.slots):
            self._load()


class Arena:
    def __init__(self, nc, nbytes):
        self.ap = nc.alloc_sbuf_tensor("arena", [128, nbytes // 4], F32).ap()
        self.nbytes = nbytes

    def f32(self, off, n):
        assert off % 4 == 0 and off + 4 * n <= self.nbytes, (off, n)
        return self.ap[:, off // 4: off // 4 + n]

    def bf(self, off, n):
        assert off % 4 == 0 and n % 2 == 0 and off + 2 * n <= self.nbytes, (off, n)
        return self.ap[:, off // 4: off // 4 + n // 2].bitcast(BF16)


class Alloc:
    def __init__(self, arena, lo, hi):
        self.a, self.lo, self.hi, self.p = arena, lo, hi, lo

    def f32(self, n):
        off = self.p
        self.p += 4 * n
        assert self.p <= self.hi, ("sbuf overflow", self.p - self.hi)
        return self.a.f32(off, n)

    def bf(self, n):
        off = self.p
        self.p += 2 * n
        self.p = (self.p + 3) // 4 * 4
        assert self.p <= self.hi, ("sbuf overflow", self.p - self.hi)
        return self.a.bf(off, n)


OFF_X = 0
OFF_CTX = OFF_X + NCH * L * 4
OFF_CONST = OFF_CTX + NCH * LC * 4
SZ_CONST = 7 * 1024
OFF_WORK = OFF_CONST + SZ_CONST
ARENA_BYTES = 212480


class K:
    pass


def build(layers, load_x=True):
    nc = bass.Bass("TRN2", target_bir_lowering=False)
    stack = ExitStack()
    P = Prog(nc, stack)
    S = K()
    S.nc, S.P = nc, P

    def din(name, shape, dt=F32):
        return nc.dram_tensor(name, list(shape), dt, kind="ExternalInput").ap()

    def dscr(name, shape, dt=BF16):
        return nc.dram_tensor(name, list(shape), dt).ap()

    I = K()
    I.xT = din("xT", [D, L])
    I.ctxT = din("ctxT", [D, LC])
    I.cc = din("cc", [128, 16])
    I.ada_w = din("ada_w", [DEPTH, D, 6 * D])
    I.ada_bT = din("ada_bT", [128, DEPTH * 48])
    I.norm_gT = din("norm_gT", [128, DEPTH * 4 * NCH])
    I.ffn_w_gu = din("ffn_w_gu", [DEPTH, D, 2 * HID])
    I.ffn_w_down = din("ffn_w_down", [DEPTH, HID, D])
    I.pool_w = din("pool_w", [2, 4, 256, 256])
    I.pool_scaleT = din("pool_scaleT", [128, 2 * NCH])
    I.pool_corr = din("pool_corr", [128, 64])
    I.mla_w_in = din("mla_w_in", [D, 768])
    I.mla_w_qb = din("mla_w_qb", [384, 2048])
    I.mla_w_kvb = din("mla_w_kvb", [256, 2048])
    I.mla_nT = din("mla_nT", [128, 5])
    I.mla_w_o = din("mla_w_o", [D, D])
    I.rope_cs = din("rope_cs", [2, 64, L])
    I.na_w_in = din("na_w_in", [D, 3 * D])
    I.na_w_o = din("na_w_o", [D, D])
    I.na_w2 = din("na_w2", [16, 128, 23 * 64])
    I.na_koh = din("na_koh", [16, T])
    I.na_bq = din("na_bq", [8, 16, 512])
    I.ident = din("ident", [128, 128])
    yT = nc.dram_tensor("yT", [D, L], F32, kind="ExternalOutput").ap()
    ctx_out = nc.dram_tensor("ctx_out", [D, LC], F32, kind="ExternalOutput").ap()
    S.I = I

    ar = Arena(nc, ARENA_BYTES)
    S.ar = ar
    S.X = ar.f32(OFF_X, NCH * L).rearrange("p (c t) -> p c t", c=NCH)
    S.CTX = ar.f32(OFF_CTX, NCH * LC).rearrange("p (c t) -> p c t", c=NCH)
    S.xb = [[[Buf() for _ in range(L // 256)] for _ in range(NCH)],
            [[Buf() for _ in range(1)] for _ in range(NCH)]]
    S.psum = [nc.alloc_psum_tensor("ps%d" % i, [128, 512], F32).ap() for i in range(8)]
    S.psb = [Buf() for _ in range(8)]

    ca = Alloc(ar, OFF_CONST, OFF_CONST + SZ_CONST)
    S.ones = ca.bf(128)
    S.ident = ca.bf(128)
    S.mods = ca.f32(DEPTH * 96).rearrange("p (l n j) -> p l n j", l=DEPTH, n=48)
    S.der = ca.f32(DEPTH * 64).rearrange("p (l q c j) -> p l q c j", l=DEPTH, q=4, c=NCH)
    S.ng = ca.f32(DEPTH * 32).rearrange("p (l i c) -> p l i c", l=DEPTH, i=4)
    S.pscale = ca.f32(16).rearrange("p (l c) -> p l c", l=2)
    S.pcorr = ca.f32(64)
    S.mlan = ca.f32(8)
    S.epst = ca.f32(4)
    S.ones32 = ca.f32(128)
    S.ident32 = ca.f32(128)
    S.epsc = {1024: S.epst[:, 0:1], 384: S.epst[:, 1:2], 256: S.epst[:, 2:3]}
    S.cbuf = Buf()

    import os
    stage = int(os.environ.get("KSTAGE", "99"))
    setup_consts(S)
    pipe = cast_all(S, layers)
    adaln_phase(S, pipe)
    if load_x:
        load_stream(S)
    for l in (layers if stage >= 3 else []):
        kind = l % 3
        ctx_after = any(m % 3 != 0 for m in range(l + 1, DEPTH))
        if kind == 0:
            pool_phase(S, l, ctx_after)
            if stage == 3:
                break
        elif kind == 1:
            mla_phase(S, l, ctx_after)
        else:
            na_phase(S, l)
        ffn_phase(S, l, ctx_after)
    store_stream(S, yT, ctx_out)
    P.emit()
    stack.close()
    return nc


def xtile_bufs(S, s, c, t0, n):
    if s == 1:
        return [S.xb[1][c][0]]
    return [S.xb[0][c][b] for b in range(t0 // 256, (t0 + n + 255) // 256)]


def xtile_ap(S, s, c, t0, n):
    return (S.X if s == 0 else S.CTX)[:, c, t0:t0 + n]


def load_stream(S):
    P, I = S.P, S.I
    for c in range(NCH):
        for t0 in range(0, L, 1024):
            P.dma(S.X[:, c, t0:t0 + 1024], I.xT[c * 128:(c + 1) * 128, t0:t0 + 1024],
                  writes=xtile_bufs(S, 0, c, t0, 1024))
        P.dma(S.CTX[:, c, :], I.ctxT[c * 128:(c + 1) * 128, :], writes=xtile_bufs(S, 1, c, 0, LC))


def store_stream(S, yT, ctx_out):
    P = S.P
    for c in range(NCH):
        for t0 in range(0, L, 1024):
            P.dma(yT[c * 128:(c + 1) * 128, t0:t0 + 1024], S.X[:, c, t0:t0 + 1024],
                  reads=xtile_bufs(S, 0, c, t0, 1024))
        P.dma(ctx_out[c * 128:(c + 1) * 128, :], S.CTX[:, c, :], reads=xtile_bufs(S, 1, c, 0, LC))


def setup_consts(S):
    P, I = S.P, S.I
    wa = Alloc(S.ar, OFF_WORK, ARENA_BYTES)
    tmp = wa.f32(128)
    tb = Buf()
    mset(P, "pool", S.ones, 1.0, [S.cbuf])
    mset(P, "pool", S.ones32, 1.0, [S.cbuf])
    for i_, n_ in enumerate((1024, 384, 256)):
        mset(P, "pool", S.epst[:, i_:i_ + 1], EPS * n_, [S.cbuf])
    P.dma(S.ident32, I.ident, writes=[S.cbuf])
    cpy(P, "dve", S.ident, S.ident32, [S.cbuf], [S.cbuf])
    P.dma(S.ng.rearrange("p l i c -> p (l i c)"), I.norm_gT, writes=[S.cbuf])
    P.dma(S.pscale.rearrange("p l c -> p (l c)"), I.pool_scaleT, writes=[S.cbuf])
    P.dma(S.pcorr, I.pool_corr, writes=[S.cbuf])
    P.dma(S.mlan[:, 0:5], I.mla_nT, writes=[S.cbuf])
    ts(P, "dve", S.mlan[:, 0:3], S.mlan[:, 0:3], float(np.sqrt(384.0)), None, ALU.mult, None, [S.cbuf], [S.cbuf])
    ts(P, "dve", S.mlan[:, 3:5], S.mlan[:, 3:5], 16.0, None, ALU.mult, None, [S.cbuf], [S.cbuf])


def adaln_phase(S, pipe=None):
    P, I = S.P, S.I
    wa = Alloc(S.ar, OFF_WORK, ARENA_BYTES)
    cc = wa.f32(16)
    sc = wa.f32(16)
    ab = wa.f32(DEPTH * 48)
    ccb, scb, abb = Buf(), Buf(), Buf()
    NW = 512
    wsl = [(wa.f32(NCH * NW), Buf()) for _ in range(3)]
    P.dma(cc, I.cc, writes=[ccb])
    P.dma(ab, I.ada_bT, writes=[abb])
    actf(P, sc, cc, AF.Silu, [ccb], [scb])
    sc3 = sc.rearrange("p (k j) -> p k j", j=2)
    ab3 = ab.rearrange("p (l n) -> p l n", l=DEPTH)
    items = [(l, nb) for l in range(DEPTH) for nb in range(12)]

    def loader(n, ap, buf):
        l, nb = items[n]
        P.dma(ap.rearrange("p (k c) -> p k c", k=NCH),
              I.ada_w[l, :, nb * NW:(nb + 1) * NW].rearrange("(k p) c -> p k c", p=128), writes=[buf])

    ring = Ring(P, wsl, len(items), loader)
    ring.prime()
    for n, (l, nb) in enumerate(items):
        w, wb = ring.get(n)
        w3 = w.rearrange("p (k c) -> p k c", k=NCH)
        ps, psb = S.psum[l % 2], S.psb[l % 2]
        for m in range(4):
            col = (nb * 4 + m) * 2
            for k in range(NCH):
                mm(P, psb, ps[:, col:col + 2], w3[:, k, m * 128:(m + 1) * 128], sc3[:, k, :], [wb, scb],
                   start=(k == 0), stop=(k == NCH - 1))
        ring.done(n)
        if pipe is not None:
            for _ in range(-(-len(pipe.items) // len(items))):
                pipe.step()
        if nb == 11:
            for j in range(2):
                tt(P, "dve", S.mods[:, l, :, j], ps[:, 0:96].rearrange("p (n j) -> p n j", j=2)[:, :, j],
                   ab3[:, l, :], ALU.add, [psb, abb], [S.cbuf])
    if pipe is not None:
        pipe.drain()
    rt = float(np.sqrt(float(D)))
    for l in range(DEPTH):
        for q, (gi, mi, plus1) in enumerate(((0, 1, True), (1, 2, False), (2, 4, True), (3, 5, False))):
            for j in range(2):
                src = S.mods[:, l, mi * 8:(mi + 1) * 8, j]
                dst = S.der[:, l, q, :, j]
                if plus1:
                    ts(P, "dve", dst, src, 1.0, rt, ALU.add, ALU.mult, [S.cbuf], [S.cbuf])
                else:
                    ts(P, "dve", dst, src, rt, None, ALU.mult, None, [S.cbuf], [S.cbuf])
                tt(P, "dve", dst, dst, S.ng[:, l, gi, :], ALU.mult, [S.cbuf], [S.cbuf])
    P.barrier()


def shiftv(S, l, which, c, j):
    return S.mods[:, l, which * 8 + c, j:j + 1]


class NormWork:
    def __init__(self, wa, N, banks, share=None):
        self.N = N
        if share is None:
            self.sq = [(wa.bf(N), Buf()) for _ in range(2)]
            self.tmp = [(wa.f32(N), Buf()) for _ in range(2)]
        else:
            self.sq, self.tmp = share.sq, share.tmp
        self.rstd = wa.f32(N)
        self.rstdb = Buf()
        self.banks = banks
        self.i = 0


def ssq_rstd(S, W, srcs, n_feat, N):
    P = S.P
    bank = W.banks[W.i % len(W.banks)]
    W.i += 1
    ps, psb = S.psum[bank], S.psb[bank]
    for c, (ap, bufs) in enumerate(srcs):
        sq, sqb = W.sq[c % 2]
        if getattr(W, "act_sq", False) and c % 2 == 1:
            actf(P, sq[:, :N], ap, AF.Square, bufs, [sqb])
        else:
            tt(P, "dve", sq[:, :N], ap, ap, ALU.mult, bufs, [sqb])
        mm(P, psb, ps[:, :N], S.ones, sq[:, :N], [sqb, S.cbuf], start=(c == 0), stop=(c == len(srcs) - 1), inc=True)
    actf(P, W.rstd[:, :N], ps[:, :N], AF.Sqrt, [psb], [W.rstdb], scale=1.0, bias=S.epsc[n_feat])
    P.op("dve", lambda e: e.reciprocal(out=W.rstd[:, :N], in_=W.rstd[:, :N]), [W.rstdb], [W.rstdb])


def normmod(S, W, srcs, n_feat, N, Afn, Bfn, outs):
    ssq_rstd(S, W, srcs, n_feat, N)
    normmod_apply(S, W, srcs, N, Afn, Bfn, outs)


def normmod_apply(S, W, srcs, N, Afn, Bfn, outs):
    P = S.P
    for c, (ap, bufs) in enumerate(srcs):
        oap, obufs = outs[c]
        if Bfn is None:
            stt(P, "dve", oap, ap, Afn(c), W.rstd[:, :N], ALU.mult, ALU.mult, bufs + [W.rstdb, S.cbuf], obufs)
        else:
            tmp, tb = W.tmp[c % 2]
            stt(P, "dve", tmp[:, :N], ap, Afn(c), W.rstd[:, :N], ALU.mult, ALU.mult,
                bufs + [W.rstdb, S.cbuf], [tb])
            actf(P, oap, tmp[:, :N], AF.Identity, [tb, S.cbuf], obufs, scale=1.0, bias=Bfn(c))


def postnorm_residual(S, W, fs, N, Gfn, s, t0, add_eng="dve"):
    P = S.P
    ssq_rstd(S, W, fs, D, N)
    for c, (ap, bufs) in enumerate(fs):
        tmp, tb = W.tmp[c % 2]
        stt(P, "dve", tmp[:, :N], ap, Gfn(c), W.rstd[:, :N], ALU.mult, ALU.mult, bufs + [W.rstdb, S.cbuf], [tb])
        xa = xtile_ap(S, s, c, t0, N)
        xb = xtile_bufs(S, s, c, t0, N)
        tt(P, add_eng, xa, xa, tmp[:, :N], ALU.add, xb + [tb], xb)


STAGE_N = 3072


class CastPipe:
    def __init__(self, S, alloc=None, nslots=7, stage_n=STAGE_N, engs=("dve", "dve", "act"), depth=5):
        self.S = S
        if alloc is None:
            alloc = Alloc(S.ar, OFF_X, OFF_CONST)
        self.stage_n = stage_n
        self.ns = nslots
        self.engs = engs
        self.items = []
        self.nl = 0
        self.nc = 0
        self.depth = depth
        self.st32 = self.st16 = None
        if alloc != "later":
            self.bind(alloc)

    def bind(self, alloc):
        self.st32 = [(alloc.f32(self.stage_n), Buf()) for _ in range(self.ns)]
        self.st16 = [(alloc.bf(self.stage_n), Buf()) for _ in range(self.ns)]

    def add(self, src, dst, shape, dbuf=None):
        self.items.append((src, dst, tuple(shape)))

    def _views(self, i):
        src, dst, shape = self.items[i]
        n = int(np.prod(shape))
        assert n <= self.stage_n
        a32, b32 = self.st32[i % self.ns]
        a16, b16 = self.st16[i % self.ns]
        v32, v16 = a32[:, :n], a16[:, :n]
        if len(shape) == 2:
            v32 = v32.rearrange("p (a b) -> p a b", a=shape[0])
            v16 = v16.rearrange("p (a b) -> p a b", a=shape[0])
        return src, dst, n, a32, b32, a16, b16, v32, v16

    def step(self):
        P = self.S.P
        if self.nl < len(self.items):
            src, dst, n, a32, b32, a16, b16, v32, v16 = self._views(self.nl)
            P.dma(v32, src, writes=[b32])
            self.nl += 1
        if self.nc < len(self.items) and (self.nl - self.nc > self.depth or self.nl == len(self.items)):
            src, dst, n, a32, b32, a16, b16, v32, v16 = self._views(self.nc)
            cpy(P, self.engs[self.nc % len(self.engs)], a16[:, :n], a32[:, :n], [b32], [b16])
            P.dma(dst, v16, reads=[b16])
            self.nc += 1

    def drain(self):
        while self.nc < len(self.items):
            self.step()


def cast_all(S, layers):
    I, nc = S.I, S.nc
    pipe = CastPipe(S)
    S.cast_item = pipe.add
    S.ffw = {}
    S.mixw = {}
    use_late = (len(layers) > 2 and layers[0] % 3 == 0)
    S.late_pipe = CastPipe(S, alloc="later", nslots=3, stage_n=1024, engs=("act",), depth=1) if use_late else None
    for li, l in enumerate(layers):
        kind = l % 3
        if kind == 0:
            pw_b = nc.dram_tensor("pw_b%d" % l, [128, 4 * 2 * 256], BF16).ap()
            pipe.add(I.pool_w[l // 3].rearrange("g (k p) c -> p (g k) c", p=128),
                     pw_b.rearrange("p (a c) -> p a c", a=8), (8, 256))
            S.mixw[l] = (pw_b, Buf())
        elif kind == 1:
            S.mixw[l] = mla_casts(S)
        else:
            S.mixw[l] = na_casts(S)
        wgu_b = nc.dram_tensor("wgu_b%d" % l, [NJ, 128, 2 * NCH * 128], BF16).ap()
        wd_b = nc.dram_tensor("wd_b%d" % l, [NCH, 128, NJ * 128], BF16).ap()
        gub = [Buf() for _ in range(NJ)]
        wdb = [Buf() for _ in range(NCH)]
        late = use_late and li >= 2
        tgt = S.late_pipe if late else pipe
        for j in range(NJ):
            for h in range(2):
                src = I.ffn_w_gu[l, :, h * HID + j * 128: h * HID + (j + 1) * 128].rearrange("(k p) c -> p k c", p=128)
                dst = wgu_b[j, :, h * 1024:(h + 1) * 1024].rearrange("p (k c) -> p k c", k=NCH)
                tgt.add(src, dst, (NCH, 128))
        for c in range(NCH):
            src = I.ffn_w_down[l, :, c * 128:(c + 1) * 128].rearrange("(j p) c -> p j c", p=128)
            dst = wd_b[c].rearrange("p (j c) -> p j c", j=NJ)
            if late:
                for j0, j1 in ((0, 8), (8, 16), (16, NJ)):
                    tgt.add(src[:, j0:j1, :], dst[:, j0:j1, :], (j1 - j0, 128))
            else:
                tgt.add(src, dst, (NJ, 128))
        S.ffw[l] = (wgu_b, wd_b, gub, wdb)
    return pipe


FFN_NT = 384


def ffn_phase(S, l, with_ctx):
    P, I, nc = S.P, S.I, S.nc
    N = FFN_NT
    wgu_b, wd_b, gub, wdb = S.ffw[l]
    wa = Alloc(S.ar, OFF_WORK, ARENA_BYTES)
    Wn = NormWork(wa, N, [6])
    Wp = NormWork(wa, N, [7], share=Wn)
    h = wa.bf(NCH * N).rearrange("p (k n) -> p k n", k=NCH)
    hb = [Buf() for _ in range(NCH)]
    act = wa.bf(NJ * N).rearrange("p (j n) -> p j n", j=NJ)
    actb = [Buf() for _ in range(NJ)]
    f_off = wa.p
    f = wa.f32(NCH * N).rearrange("p (c n) -> p c n", c=NCH)
    fb = [Buf() for _ in range(NCH)]
    sqn = [S.ar.bf(f_off + c * N * 4, N) for c in range(NCH)]
    sg = [(wa.bf(N), Buf()) for _ in range(2)]
    gu_slots = [(wa.bf(NCH * 128), Buf()) for _ in range(5)]
    NJH = NJ // 2
    wd_slots = [(wa.bf(NJH * 128), Buf()) for _ in range(4)]

    tiles = []
    t0 = 0
    while t0 < L:
        n = min(N, L - t0)
        tiles.append((0, t0, n))
        t0 += n
    if with_ctx:
        tiles.append((1, 0, LC))
    nt = len(tiles)

    def gu_loader(n, ap, buf):
        j, hh = (n % (2 * NJ)) // 2, n % 2
        P.dma(ap, wgu_b[j, :, hh * 1024:(hh + 1) * 1024], reads=[gub[j]], writes=[buf])

    def wd_loader(n, ap, buf):
        c, half = (n % (2 * NCH)) // 2, n % 2
        P.dma(ap, wd_b[c, :, half * NJH * 128:(half + 1) * NJH * 128], reads=[wdb[c]], writes=[buf])

    gring = Ring(P, gu_slots, nt * NJ * 2, gu_loader)
    dring = Ring(P, wd_slots, nt * NCH * 2, wd_loader)
    gring.prime()
    dring.prime()
    G_BANKS = (0, 1)
    U_BANKS = (2, 3)
    F_BANKS = (4, 5)

    def norm_args(ti):
        s, t0, n = tiles[ti]
        srcs = [(xtile_ap(S, s, c, t0, n), xtile_bufs(S, s, c, t0, n)) for c in range(NCH)]
        outs = [(h[:, c, :n], [hb[c]]) for c in range(NCH)]
        return s, n, srcs, outs

    def norm1_squares(ti):
        s, n, srcs, outs = norm_args(ti)
        for c, (ap, bufs) in enumerate(srcs):
            tt(P, "dve", sqn[c][:, :n], ap, ap, ALU.mult, bufs, [fb[c]])

    def norm1_mm(ti):
        s, n, srcs, outs = norm_args(ti)
        ps, psb = S.psum[6], S.psb[6]
        for c in range(NCH):
            mm(P, psb, ps[:, :n], S.ones, sqn[c][:, :n], [fb[c], S.cbuf], start=(c == 0), stop=(c == NCH - 1))
        actf(P, Wn.rstd[:, :n], ps[:, :n], AF.Sqrt, [psb], [Wn.rstdb], scale=1.0, bias=S.epsc[D])
        P.op("dve", lambda e: e.reciprocal(out=Wn.rstd[:, :n], in_=Wn.rstd[:, :n]), [Wn.rstdb], [Wn.rstdb])

    def do_norm2(ti):
        s, n, srcs, outs = norm_args(ti)
        normmod_apply(S, Wn, srcs, n, lambda c: S.der[:, l, 2, c, s:s + 1], lambda c: shiftv(S, l, 3, c, s), outs)

    norm1_squares(0)
    norm1_mm(0)
    do_norm2(0)
    deferred = []
    for ti, (s, t0, n) in enumerate(tiles):
        for j in range(NJ):
            it = ti * NJ + j
            pss = []
            for hh in range(2):
                w, wb = gring.get(it * 2 + hh)
                w3 = w.rearrange("p (k c) -> p k c", k=NCH)
                bank = (G_BANKS, U_BANKS)[hh][it % 2]
                ps, psb = S.psum[bank], S.psb[bank]
                for k in range(NCH):
                    mm(P, psb, ps[:, :n], w3[:, k, :], h[:, k, :n], [wb, hb[k]], start=(k == 0), stop=(k == NCH - 1))
                gring.done(it * 2 + hh)
                pss.append((ps, psb))
            sga, sgb = sg[it % 2]
            actf(P, sga[:, :n], pss[0][0][:, :n], AF.Silu, [pss[0][1]], [sgb])
            tt(P, "dve", act[:, j, :n], sga[:, :n], pss[1][0][:, :n], ALU.mult, [sgb, pss[1][1]], [actb[j]])
            if deferred and 1 <= j <= NCH:
                deferred.pop(0)()
            if ti + 1 < nt:
                if j == 10:
                    norm1_squares(ti + 1)
                if j == 17:
                    norm1_mm(ti + 1)
        if ti + 1 < nt:
            do_norm2(ti + 1)
        ps_s, ps_sb = S.psum[7], S.psb[7]

        def ssq_mm(c):
            sq, sqb = Wn.sq[c % 2]
            mm(P, ps_sb, ps_s[:, :n], S.ones, sq[:, :n], [sqb, S.cbuf], start=(c == 0), stop=(c == NCH - 1), inc=True)

        for c in range(NCH):
            it = ti * NCH + c
            bank = F_BANKS[it % 2]
            ps, psb = S.psum[bank], S.psb[bank]
            for half in range(2):
                w, wb = dring.get(it * 2 + half)
                w3 = w.rearrange("p (j c) -> p j c", j=NJH)
                for jj in range(NJH):
                    j = half * NJH + jj
                    mm(P, psb, ps[:, :n], w3[:, jj, :], act[:, j, :n], [wb, actb[j]],
                       start=(j == 0), stop=(j == NJ - 1))
                dring.done(it * 2 + half)
            cpy(P, "act", f[:, c, :n], ps[:, :n], [psb], [fb[c]])
            sq, sqb = Wn.sq[c % 2]
            tt(P, "dve", sq[:, :n], f[:, c, :n], f[:, c, :n], ALU.mult, [fb[c]], [sqb])
            if c >= 1:
                ssq_mm(c - 1)
        ssq_mm(NCH - 1)
        actf(P, Wp.rstd[:, :n], ps_s[:, :n], AF.Sqrt, [ps_sb], [Wp.rstdb], scale=1.0, bias=S.epsc[D])
        P.op("dve", lambda e, n=n: e.reciprocal(out=Wp.rstd[:, :n], in_=Wp.rstd[:, :n]), [Wp.rstdb], [Wp.rstdb])
        def upd(c, s=s, t0=t0, n=n):
            stt(P, "dve", f[:, c, :n], f[:, c, :n], S.der[:, l, 3, c, s:s + 1], Wp.rstd[:, :n], ALU.mult, ALU.mult,
                [fb[c], Wp.rstdb, S.cbuf], [fb[c]])
            xa = xtile_ap(S, s, c, t0, n)
            xbf = xtile_bufs(S, s, c, t0, n)
            tt(P, "dve", xa, xa, f[:, c, :n], ALU.add, xbf + [fb[c]], xbf)

        deferred = [(lambda c=c, u=upd: u(c)) for c in range(NCH)]
    for d_ in deferred:
        d_()
    P.barrier()


def pool_phase(S, l, ctx_after):
    P, I, nc = S.P, S.I, S.nc
    jl = l // 3
    N = 256
    PADW = N + 32
    wa = Alloc(S.ar, OFF_WORK, ARENA_BYTES)
    W = NormWork(wa, PADW, [7])
    Wp = NormWork(wa, N, [6])
    W.act_sq = Wp.act_sq = False
    pw_b, pwdb = S.mixw[l]
    pw = wa.bf(4 * 2 * 256)
    pwb = Buf()
    P.dma(pw, pw_b, reads=[pwdb], writes=[pwb])
    pw4 = pw.rearrange("p (g k c) -> p g k c", g=4, k=2)
    hp = wa.f32(NCH * PADW).rearrange("p (c u) -> p c u", c=NCH)
    hpb = [Buf() for _ in range(NCH)]
    sa = [(wa.f32(PADW), Buf()) for _ in range(2)]
    sb_ = [(wa.f32(PADW), Buf()) for _ in range(2)]
    pooled2 = [(wa.bf(NCH * N).rearrange("p (c n) -> p c n", c=NCH), [Buf() for _ in range(NCH)]) for _ in range(2)]
    y = wa.f32(NCH * N).rearrange("p (c n) -> p c n", c=NCH)
    yb = [Buf() for _ in range(NCH)]
    tiles = [(0, t0) for t0 in range(0, L, N)]
    if ctx_after:
        tiles.append((1, 0))
    OFFU = 16

    late = S.late_pipe if (S.late_pipe is not None and S.late_pipe.st32 is None) else None
    if late is not None:
        late.bind(wa)
        late.depth = 2
    nticks = [0]
    ticks_per_tile = 16
    steps_total = (len(late.items) + 3) if late is not None else 0
    every = max(1, (ticks_per_tile * len(tiles)) // max(1, steps_total))

    def tick():
        if late is None:
            return
        nticks[0] += 1
        if nticks[0] % every == 0:
            late.step()

    def stage_a(ti):
        s, t0 = tiles[ti]
        pooled, pob = pooled2[ti % 2]
        Ls = L if s == 0 else LC
        a = t0
        b = min(t0 + N + 8, Ls)
        n = b - a
        u0 = OFFU
        first = (t0 == 0)
        last = (t0 + N == Ls)
        for c in range(NCH):
            eng = "pool" if c % 2 == 0 else "dve"
            if first:
                mset(P, eng, hp[:, c, 0:OFFU], 0.0, [hpb[c]])
            else:
                cpy(P, eng, hp[:, c, 8:16], hp[:, c, N + 8:N + 16], [hpb[c]], [hpb[c]])
            if last:
                mset(P, eng, hp[:, c, OFFU + N:PADW], 0.0, [hpb[c]])
        srcs = [(xtile_ap(S, s, c, a, n), xtile_bufs(S, s, c, a, n)) for c in range(NCH)]
        outs = [(hp[:, c, u0:u0 + n], [hpb[c]]) for c in range(NCH)]
        normmod(S, W, srcs, D, n, lambda c: S.der[:, l, 0, c, s:s + 1], lambda c: shiftv(S, l, 0, c, s), outs)
        for c in range(NCH):
            g = c // 2
            win = (2, 4, 8, 16)[g]
            hc = hp[:, c, :]
            A_, Ab = sa[c % 2]
            B_, Bb = sb_[c % 2]
            eng = "pool" if c % 2 == 0 else "dve"
            tt(P, eng, A_[:, 9:280], hc[:, 8:279], hc[:, 9:280], ALU.add, [hpb[c]], [Ab])
            cur, curb = A_, Ab
            if g >= 1:
                tt(P, eng, B_[:, 10:279], A_[:, 9:278], A_[:, 11:280], ALU.add, [Ab], [Bb])
                cur, curb = B_, Bb
            if g >= 2:
                tt(P, eng, A_[:, 12:277], B_[:, 10:275], B_[:, 14:279], ALU.add, [Bb], [Ab])
                cur, curb = A_, Ab
            if g >= 3:
                tt(P, eng, B_[:, 16:273], A_[:, 12:269], A_[:, 20:277], ALU.add, [Ab], [Bb])
                cur, curb = B_, Bb
            if first:
                tt(P, eng, cur[:, OFFU:OFFU + 8], cur[:, OFFU:OFFU + 8], S.pcorr[:, g * 16: g * 16 + 8], ALU.mult,
                   [curb, S.cbuf], [curb])
            if last:
                tt(P, eng, cur[:, OFFU + N - 8:OFFU + N], cur[:, OFFU + N - 8:OFFU + N],
                   S.pcorr[:, g * 16 + 8: g * 16 + 16], ALU.mult, [curb, S.cbuf], [curb])
            stt(P, "dve", pooled[:, c, :], cur[:, OFFU:OFFU + N], 1.0 / win, hc[:, OFFU:OFFU + N],
                ALU.mult, ALU.subtract, [curb, hpb[c]], [pob[c]])
            tick()

    def stage_b(ti):
        s, t0 = tiles[ti]
        pooled, pob = pooled2[ti % 2]
        for g in range(4):
            for m in range(2):
                bank = (g * 2 + m) % 4
                ps, psb = S.psum[bank], S.psb[bank]
                for k in range(2):
                    mm(P, psb, ps[:, :N], pw4[:, g, k, m * 128:(m + 1) * 128], pooled[:, 2 * g + k, :],
                       [pwb, pob[2 * g + k]], start=(k == 0), stop=(k == 1))
                c = 2 * g + m
                actf(P, y[:, c, :], ps[:, :N], AF.Identity, [psb, S.cbuf], [yb[c]], scale=S.pscale[:, jl, c:c + 1])
                tick()
        fs = [(y[:, c, :], [yb[c]]) for c in range(NCH)]
        postnorm_residual(S, Wp, fs, N, lambda c: S.der[:, l, 1, c, s:s + 1], s, t0, add_eng="pool")

    for ti in range(len(tiles)):
        stage_a(ti)
        stage_b(ti)
    if late is not None:
        late.drain()
    P.barrier()


def _cast_rows(S, src2d, dst, nk, ncol):
    for k in range(nk):
        S.cast_item(src2d[k * 128:(k + 1) * 128, :], dst[:, k * ncol:(k + 1) * ncol], (ncol,), None)


def mla_casts(S):
    I, nc = S.I, S.nc
    w = K()
    w.win = nc.dram_tensor("mla_win_b", [128, 8 * 768], BF16).ap()
    w.wqb = nc.dram_tensor("mla_wqb_b", [128, 3 * 2048], BF16).ap()
    w.wkvb = nc.dram_tensor("mla_wkvb_b", [128, 2 * 2048], BF16).ap()
    w.wo = nc.dram_tensor("mla_wo_b", [128, 8 * 1024], BF16).ap()
    _cast_rows(S, I.mla_w_in, w.win, 8, 768)
    _cast_rows(S, I.mla_w_qb, w.wqb, 3, 2048)
    _cast_rows(S, I.mla_w_kvb, w.wkvb, 2, 2048)
    _cast_rows(S, I.mla_w_o, w.wo, 8, 1024)
    return w


def na_casts(S):
    I, nc = S.I, S.nc
    w = K()
    w.win = nc.dram_tensor("na_win_b", [6, 128, 8 * 512], BF16).ap()
    w.wo = nc.dram_tensor("na_wo_b", [128, 8 * 1024], BF16).ap()
    w.w2 = nc.dram_tensor("na_w2_b", [16, 128, 23 * 64], BF16).ap()
    for k in range(8):
        S.cast_item(I.na_w_in[k * 128:(k + 1) * 128, :].rearrange("p (i c) -> p i c", i=6),
                    w.win[:, :, k * 512:(k + 1) * 512].rearrange("i p c -> p i c"), (6, 512), None)
    _cast_rows(S, I.na_w_o, w.wo, 8, 1024)
    for h in range(16):
        S.cast_item(I.na_w2[h], w.w2[h], (23 * 64,), None)
    return w


def attn_tiles(with_ctx, N):
    tiles = []
    if with_ctx:
        tiles.append((1, 0, LC, 0))
    for t0 in range(0, L, N):
        tiles.append((0, t0, N, LC + t0))
    return tiles


def mla_phase(S, l, ctx_after):
    P, I, nc = S.P, S.I, S.nc
    w = S.mixw[l]
    N = 256
    qn_d = nc.dram_tensor("mla_qn_d", [3, 128, T], BF16).ap()
    kvn_d = nc.dram_tensor("mla_kvn_d", [2, 128, T], BF16).ap()
    kr_d = nc.dram_tensor("mla_kr_d", [64, T], BF16).ap()
    q_d = nc.dram_tensor("mla_q_d", [8, 128, T], BF16).ap()
    qr_d = nc.dram_tensor("mla_qr_d", [8, 64, T], BF16).ap()
    k_d = nc.dram_tensor("mla_k_d", [8, 128, T], BF16).ap()
    v_d = nc.dram_tensor("mla_v_d", [T, 1024], BF16).ap()
    o_d = nc.dram_tensor("mla_o_d", [8, 128, T], BF16).ap()
    tiles = attn_tiles(True, N)

    NA_ = 512
    tiles_a = attn_tiles(True, NA_)
    wa = Alloc(S.ar, OFF_WORK, ARENA_BYTES)
    W = NormWork(wa, NA_, [6, 7])
    win = wa.bf(8 * 768)
    winb = Buf()
    P.dma(win, w.win, writes=[winb])
    win3 = win.rearrange("p (k c) -> p k c", k=8)
    h = wa.bf(NCH * NA_).rearrange("p (k n) -> p k n", k=NCH)
    hb = [Buf() for _ in range(NCH)]
    aa = wa.f32(5 * NA_).rearrange("p (m n) -> p m n", m=5)
    aab = [Buf() for _ in range(5)]
    qns = [(wa.bf(3 * NA_).rearrange("p (m n) -> p m n", m=3), [Buf() for _ in range(3)]) for _ in range(2)]
    kvs = [(wa.bf(2 * NA_).rearrange("p (m n) -> p m n", m=2), [Buf() for _ in range(2)]) for _ in range(2)]
    tabs = [(wa.f32(2 * NA_).rearrange("p (a n) -> p a n", a=2), Buf()) for _ in range(2)]
    t1 = wa.f32(NA_)
    t2 = wa.f32(NA_)
    t1b, t2b = Buf(), Buf()
    krs = [(wa.bf(NA_), Buf()) for _ in range(2)]
    for ti, (s, t0, n, T0) in enumerate(tiles_a):
        srcs = [(xtile_ap(S, s, c, t0, n), xtile_bufs(S, s, c, t0, n)) for c in range(NCH)]
        outs = [(h[:, c, :n], [hb[c]]) for c in range(NCH)]
        normmod(S, W, srcs, D, n, lambda c: S.der[:, l, 0, c, s:s + 1], lambda c: shiftv(S, l, 0, c, s), outs)
        if s == 0:
            tab, tabb = tabs[ti % 2]
            P.dma(tab[0:64, :, :n], I.rope_cs[:, :, t0:t0 + n].rearrange("a p t -> p a t"), writes=[tabb])
        for m in range(5):
            bank = m % 4
            ps, psb = S.psum[bank], S.psb[bank]
            for k in range(NCH):
                mm(P, psb, ps[:, :n], win3[:, k, m * 128:(m + 1) * 128], h[:, k, :n], [winb, hb[k]],
                   start=(k == 0), stop=(k == NCH - 1))
            cpy(P, "act", aa[:, m, :n], ps[:, :n], [psb], [aab[m]])
        ps, psb = S.psum[4], S.psb[4]
        ps2, ps2b = S.psum[5], S.psb[5]
        for hh in range(2):
            for k in range(NCH):
                mm(P, (psb, ps2b)[hh], (ps, ps2)[hh][0:64, 0:n], win3[:, k, 640 + hh * 64:704 + hh * 64], h[:, k, :n],
                   [winb, hb[k]], start=(k == 0), stop=(k == NCH - 1))
        qn, qnb = qns[ti % 2]
        kv, kvb = kvs[ti % 2]
        normmod(S, W, [(aa[:, m, :n], [aab[m]]) for m in range(3)], 384, n, lambda c: S.mlan[:, c:c + 1], None,
                [(qn[:, m, :n], [qnb[m]]) for m in range(3)])
        normmod(S, W, [(aa[:, 3 + m, :n], [aab[3 + m]]) for m in range(2)], 256, n,
                lambda c: S.mlan[:, 3 + c:4 + c], None, [(kv[:, m, :n], [kvb[m]]) for m in range(2)])
        kr, krb = krs[ti % 2]
        if s == 0:
            tt(P, "dve", t1[0:64, :n], ps[0:64, 0:n], tab[0:64, 0, :n], ALU.mult, [psb, tabb], [t1b])
            tt(P, "dve", t2[0:64, :n], ps2[0:64, 0:n], tab[0:64, 1, :n], ALU.mult, [ps2b, tabb], [t2b])
            tt(P, "pool", kr[0:64, :n], t1[0:64, :n], t2[0:64, :n], ALU.add, [t1b, t2b], [krb])
        else:
            cpy(P, "act", kr[0:64, :n], ps[0:64, 0:n], [psb], [krb])
        P.dma(qn_d[:, :, T0:T0 + n].rearrange("m p t -> p m t"), qn[:, :, :n], reads=qnb)
        P.dma(kvn_d[:, :, T0:T0 + n].rearrange("m p t -> p m t"), kv[:, :, :n], reads=kvb)
        P.dma(kr_d[:, T0:T0 + n], kr[0:64, :n], reads=[krb])
    P.barrier()

    wa = Alloc(S.ar, OFF_WORK, ARENA_BYTES)
    wqb = wa.bf(3 * 2048)
    wkvb = wa.bf(2 * 2048)
    wqbb, wkvbb = Buf(), Buf()
    P.dma(wqb, w.wqb, writes=[wqbb])
    P.dma(wkvb, w.wkvb, writes=[wkvbb])
    wqb3 = wqb.rearrange("p (k c) -> p k c", k=3)
    wkvb3 = wkvb.rearrange("p (k c) -> p k c", k=2)
    qnl = [(wa.bf(3 * N).rearrange("p (m n) -> p m n", m=3), Buf()) for _ in range(2)]
    kvl = [(wa.bf(2 * N).rearrange("p (m n) -> p m n", m=2), Buf()) for _ in range(2)]
    tabs = [(wa.f32(2 * N).rearrange("p (a n) -> p a n", a=2), Buf()) for _ in range(2)]
    Qs = [(wa.bf(8 * N).rearrange("p (h n) -> p h n", h=8), Buf()) for _ in range(2)]
    Ks = [(wa.bf(8 * N).rearrange("p (h n) -> p h n", h=8), Buf()) for _ in range(2)]
    Qrs = [(wa.bf(8 * N).rearrange("p (h n) -> p h n", h=8), Buf()) for _ in range(1)]
    Vs = [(wa.bf(2 * 1024), Buf()) for _ in range(1)]
    tq = [(wa.f32(N), Buf()) for _ in range(4)]

    def load_tile(ti):
        s, t0, n, T0 = tiles[ti]
        P.dma(qnl[ti % 2][0][:, :, :n], qn_d[:, :, T0:T0 + n].rearrange("m p t -> p m t"), writes=[qnl[ti % 2][1]])
        P.dma(kvl[ti % 2][0][:, :, :n], kvn_d[:, :, T0:T0 + n].rearrange("m p t -> p m t"), writes=[kvl[ti % 2][1]])
        if s == 0:
            P.dma(tabs[ti % 2][0][0:64, :, :n], I.rope_cs[:, :, t0:t0 + n].rearrange("a p t -> p a t"),
                  writes=[tabs[ti % 2][1]])

    load_tile(0)
    nps = 0
    for ti, (s, t0, n, T0) in enumerate(tiles):
        if ti + 1 < len(tiles):
            load_tile(ti + 1)
        qn, qnb = qnl[ti % 2]
        kv, kvb = kvl[ti % 2]
        tab, tabb = tabs[ti % 2]
        Q, Qb = Qs[ti % 2]
        Kk, Kb = Ks[ti % 2]
        Qr, Qrb = Qrs[0]
        V, Vb = Vs[0]
        for hd in range(8):
            ps, psb = S.psum[nps % 6], S.psb[nps % 6]
            nps += 1
            for k in range(3):
                mm(P, psb, ps[:, :n], wqb3[:, k, hd * 128:(hd + 1) * 128], qn[:, k, :n], [wqbb, qnb],
                   start=(k == 0), stop=(k == 2))
            actf(P, Q[:, hd, :n], ps[:, :n], AF.Identity, [psb], [Qb], scale=MLA_SCALE)
            ps, psb = S.psum[nps % 6], S.psb[nps % 6]
            nps += 1
            for hh in range(2):
                for k in range(3):
                    c0 = 1024 + hh * 512 + hd * 64
                    mm(P, psb, ps[0:64, hh * 256:hh * 256 + n], wqb3[:, k, c0:c0 + 64], qn[:, k, :n], [wqbb, qnb],
                       start=(k == 0), stop=(k == 2))
            if s == 0:
                (ta, tab_), (tb_, tbb_) = tq[(hd % 2) * 2], tq[(hd % 2) * 2 + 1]
                stt(P, "dve", ta[0:64, :n], ps[0:64, 0:n], MLA_SCALE, tab[0:64, 0, :n], ALU.mult, ALU.mult,
                    [psb, tabb], [tab_])
                stt(P, "dve", tb_[0:64, :n], ps[0:64, 256:256 + n], MLA_SCALE, tab[0:64, 1, :n], ALU.mult, ALU.mult,
                    [psb, tabb], [tbb_])
                tt(P, "pool", Qr[0:64, hd, :n], ta[0:64, :n], tb_[0:64, :n], ALU.add, [tab_, tbb_], [Qrb])
            else:
                actf(P, Qr[0:64, hd, :n], ps[0:64, 0:n], AF.Identity, [psb], [Qrb], scale=MLA_SCALE)
            ps, psb = S.psum[nps % 6], S.psb[nps % 6]
            nps += 1
            for k in range(2):
                mm(P, psb, ps[:, :n], wkvb3[:, k, hd * 128:(hd + 1) * 128], kv[:, k, :n], [wkvbb, kvb],
                   start=(k == 0), stop=(k == 1))
            cpy(P, "act" if hd % 2 == 0 else "dve", Kk[:, hd, :n], ps[:, :n], [psb], [Kb])
        for tb in range(n // 128):
            for half in range(2):
                ps, psb = S.psum[6 + (tb * 2 + half) % 2], S.psb[6 + (tb * 2 + half) % 2]
                for k in range(2):
                    mm(P, psb, ps[:, :], kv[:, k, tb * 128:(tb + 1) * 128],
                       wkvb3[:, k, 1024 + half * 512:1024 + (half + 1) * 512], [wkvbb, kvb],
                       start=(k == 0), stop=(k == 1))
                cpy(P, "dve" if half == 0 else "act", V[:, tb * 1024 + half * 512: tb * 1024 + (half + 1) * 512],
                    ps[:, :], [psb], [Vb])
        P.dma(q_d[:, :, T0:T0 + n].rearrange("h p t -> p h t"), Q[:, :, :n], reads=[Qb])
        P.dma(qr_d[:, :, T0:T0 + n].rearrange("h p t -> p h t"), Qr[0:64, :, :n], reads=[Qrb])
        P.dma(k_d[:, :, T0:T0 + n].rearrange("h p t -> p h t"), Kk[:, :, :n], reads=[Kb])
        P.dma(v_d[T0:T0 + n, :].rearrange("(tb p) c -> p tb c", p=128),
              V[:, :(n // 128) * 1024].rearrange("p (tb c) -> p tb c", c=1024), reads=[Vb])
    P.barrier()

    wa = Alloc(S.ar, OFF_WORK, ARENA_BYTES)
    NKC = T // 128
    krt = wa.bf(T)
    krtb = Buf()
    mset(P, "pool", krt[64:128, :], 0.0, [krtb])
    P.dma(krt[0:64, :], kr_d, writes=[krtb])
    Kh = [(wa.bf(T), Buf()) for _ in range(2)]
    Vh = [(wa.bf(NKC * 128).rearrange("p (kc d) -> p kc d", d=128), Buf()) for _ in range(2)]
    qt = [(wa.bf(512), wa.bf(512), Buf()) for _ in range(2)]
    NPT = 6
    Pt = [(wa.bf(512), Buf()) for _ in range(NPT)]
    sums = [(wa.bf(512), Buf()) for _ in range(6)]
    rl = wa.f32(512)
    rlb = Buf()
    ost = [(wa.bf(512), Buf()) for _ in range(2)]
    for i_ in range(2):
        mset(P, "dve", qt[i_][1][64:128, :], 0.0, [qt[i_][2]])
    qtiles = []
    if ctx_after:
        qtiles.append((0, LC, [0, 1]))
    for i in range(L // 512):
        qtiles.append((LC + i * 512, 512, list(range(NKC))))

    def load_head(hd):
        P.dma(Kh[hd % 2][0], k_d[hd], writes=[Kh[hd % 2][1]])
        vv = v_d[:, hd * 128:(hd + 1) * 128].rearrange("(kc p) d -> p kc d", p=128)
        P.dma(Vh[hd % 2][0][:, 0:17, :], vv[:, 0:17, :], writes=[Vh[hd % 2][1]])
        P.dma(Vh[hd % 2][0][:, 17:34, :], vv[:, 17:34, :], writes=[Vh[hd % 2][1]])

    def load_q(hd, qi):
        T0, n, _ = qtiles[qi]
        g = hd * len(qtiles) + qi
        qa_, qr_, qb_ = qt[g % 2]
        P.dma(qa_[:, :n], q_d[hd, :, T0:T0 + n], writes=[qb_])
        P.dma(qr_[0:64, :n], qr_d[hd, :, T0:T0 + n], writes=[qb_])

    load_head(0)
    load_q(0, 0)
    nS = 0
    nO = 0
    ngrp = 0
    for hd in range(8):
        if hd + 1 < 8:
            load_head(hd + 1)
        Kt, Ktb = Kh[hd % 2]
        Vt, Vtb = Vh[hd % 2]
        for qi, (T0, n, kcs) in enumerate(qtiles):
            g = hd * len(qtiles) + qi
            if g + 1 < 8 * len(qtiles):
                load_q((g + 1) // len(qtiles), (g + 1) % len(qtiles))
            qa_, qr_, qb_ = qt[g % 2]
            ob = 4 + nO % 2
            lb = 6 + nO % 2
            nO += 1
            pso, psob = S.psum[ob], S.psb[ob]
            psl, pslb = S.psum[lb], S.psb[lb]

            def issue_S(i):
                kc = kcs[i]
                bank = (nS + i) % 4
                ps, psb = S.psum[bank], S.psb[bank]
                mm(P, psb, ps[:, :n], Kt[:, kc * 128:(kc + 1) * 128], qa_[:, :n], [Ktb, qb_], start=True, stop=False)
                mm(P, psb, ps[:, :n], krt[:, kc * 128:(kc + 1) * 128], qr_[:, :n], [krtb, qb_],
                   start=False, stop=True)

            issue_S(0)
            if len(kcs) > 1:
                issue_S(1)
            grp, grp2, first_l, pend = [], [], True, None
            for i, kc in enumerate(kcs):
                if i + 2 < len(kcs):
                    issue_S(i + 2)
                bank = (nS + i) % 4
                ps, psb = S.psum[bank], S.psb[bank]
                pt, ptb = Pt[(nS + i) % NPT]
                actf(P, pt[:, :n], ps[:, :n], AF.Exp, [psb], [ptb])
                last = (i == len(kcs) - 1)
                mm(P, psob, pso[:, :n], Vt[:, kc, :], pt[:, :n], [Vtb, ptb], start=(i == 0), stop=last, inc=True)
                grp.append((pt, ptb))
                if len(grp) == 2:
                    sa_, sab_ = sums[(ngrp % 2) * 3 + len(grp2)]
                    tt(P, "dve", sa_[:, :n], grp[0][0][:, :n], grp[1][0][:, :n], ALU.add, [grp[0][1], grp[1][1]], [sab_])
                    grp2.append((sa_, sab_))
                    grp = []
                if len(grp2) == 2 or last:
                    parts = grp2 + grp
                    if len(parts) == 2:
                        sb_, sbb_ = sums[(ngrp % 2) * 3 + 2]
                        tt(P, "dve", sb_[:, :n], parts[0][0][:, :n], parts[1][0][:, :n], ALU.add,
                           [parts[0][1], parts[1][1]], [sbb_])
                        tot, totb = sb_, sbb_
                    else:
                        tot, totb = parts[0]
                    if pend is not None:
                        mm(P, pslb, psl[:, :n], S.ones, pend[0][:, :n], [S.cbuf, pend[1]], start=first_l, stop=False,
                           inc=True)
                        first_l = False
                        pend = None
                    if last:
                        mm(P, pslb, psl[:, :n], S.ones, tot[:, :n], [S.cbuf, totb], start=first_l, stop=True, inc=True)
                    else:
                        pend = (tot, totb)
                    ngrp += 1
                    grp, grp2 = [], []
                elif pend is not None and len(grp) == 0 and len(grp2) == 1:
                    mm(P, pslb, psl[:, :n], S.ones, pend[0][:, :n], [S.cbuf, pend[1]], start=first_l, stop=False,
                       inc=True)
                    first_l = False
                    pend = None
            nS += len(kcs)
            P.op("dve", lambda e, o=rl[:, :n], i_=psl[:, :n]: e.reciprocal(out=o, in_=i_), [pslb], [rlb])
            oo, oob = ost[g % 2]
            tt(P, "dve", oo[:, :n], pso[:, :n], rl[:, :n], ALU.mult, [psob, rlb], [oob])
            P.dma(o_d[hd, :, T0:T0 + n], oo[:, :n], reads=[oob])
    P.barrier()
    outproj_phase(S, l, o_d.rearrange("h p t -> p h t"), w.wo, attn_tiles(ctx_after, 512))


def outproj_phase(S, l, o_view, wo_b, tiles):
    P = S.P
    N = 512
    wa = Alloc(S.ar, OFF_WORK, ARENA_BYTES)
    W = NormWork(wa, N, [6, 7])
    wo = wa.bf(8 * 1024)
    wob = Buf()
    P.dma(wo, wo_b, writes=[wob])
    wo3 = wo.rearrange("p (k c) -> p k c", k=8)
    Ot = [(wa.bf(8 * N).rearrange("p (h n) -> p h n", h=8), Buf()) for _ in range(2)]
    y = wa.f32(NCH * N).rearrange("p (c n) -> p c n", c=NCH)
    yb = [Buf() for _ in range(NCH)]

    def load(ti):
        s, t0, n, T0 = tiles[ti]
        P.dma(Ot[ti % 2][0][:, :, :n], o_view[:, :, T0:T0 + n], writes=[Ot[ti % 2][1]])

    load(0)
    for ti, (s, t0, n, T0) in enumerate(tiles):
        if ti + 1 < len(tiles):
            load(ti + 1)
        O, Ob = Ot[ti % 2]
        for c in range(NCH):
            bank = c % 4
            ps, psb = S.psum[bank], S.psb[bank]
            for k in range(8):
                mm(P, psb, ps[:, :n], wo3[:, k, c * 128:(c + 1) * 128], O[:, k, :n], [wob, Ob],
                   start=(k == 0), stop=(k == 7))
            cpy(P, "act", y[:, c, :n], ps[:, :n], [psb], [yb[c]])
        fs = [(y[:, c, :n], [yb[c]]) for c in range(NCH)]
        postnorm_residual(S, W, fs, n, lambda c: S.der[:, l, 1, c, s:s + 1], s, t0)
    P.barrier()


def na_phase(S, l):
    P, I, nc = S.P, S.I, S.nc
    w = S.mixw[l]
    N = 512
    q_d = nc.dram_tensor("na_q_d", [1024, T], BF16).ap()
    k_d = nc.dram_tensor("na_k_d", [1024, T], BF16).ap()
    v_d = nc.dram_tensor("na_v_d", [T, 1024], BF16).ap()
    o_d = nc.dram_tensor("na_o_d", [1024, T], BF16).ap()
    tiles = attn_tiles(True, N)

    wa = Alloc(S.ar, OFF_WORK, ARENA_BYTES)
    W = NormWork(wa, N, [6, 7])
    h = wa.bf(NCH * N).rearrange("p (k n) -> p k n", k=NCH)
    hb = [Buf() for _ in range(NCH)]
    slots = [(wa.bf(8 * 512), Buf()) for _ in range(2)]
    Qs = [(wa.bf(8 * N).rearrange("p (h n) -> p h n", h=8), Buf()) for _ in range(1)]
    Ks = [(wa.bf(8 * N).rearrange("p (h n) -> p h n", h=8), Buf()) for _ in range(1)]
    Vs = [(wa.bf((N // 128) * 1024), Buf()) for _ in range(1)]
    items = []
    for ti, (s, t0, n, T0) in enumerate(tiles):
        for i in range(6):
            if s == 1 and i < 2:
                continue
            items.append((ti, i))

    def loader(nn, ap, buf):
        P.dma(ap, w.win[items[nn][1]], writes=[buf])

    ring = Ring(P, slots, len(items), loader)
    ring.prime()
    it = 0
    nps = 0
    for ti, (s, t0, n, T0) in enumerate(tiles):
        srcs = [(xtile_ap(S, s, c, t0, n), xtile_bufs(S, s, c, t0, n)) for c in range(NCH)]
        outs = [(h[:, c, :n], [hb[c]]) for c in range(NCH)]
        normmod(S, W, srcs, D, n, lambda c: S.der[:, l, 0, c, s:s + 1], lambda c: shiftv(S, l, 0, c, s), outs)
        Q, Qb = Qs[0]
        Kk, Kb = Ks[0]
        V, Vb = Vs[0]
        while it < len(items) and items[it][0] == ti:
            i = items[it][1]
            ws, wsb = ring.get(it)
            w3 = ws.rearrange("p (k c) -> p k c", k=8)
            if i < 4:
                for m in range(4):
                    ps, psb = S.psum[nps % 6], S.psb[nps % 6]
                    nps += 1
                    for k in range(NCH):
                        mm(P, psb, ps[:, :n], w3[:, k, m * 128:(m + 1) * 128], h[:, k, :n], [wsb, hb[k]],
                           start=(k == 0), stop=(k == NCH - 1))
                    pm = (i % 2) * 4 + m
                    if i < 2:
                        actf(P, Q[:, pm, :n], ps[:, :n], AF.Identity, [psb], [Qb], scale=NA_SCALE)
                    else:
                        cpy(P, "dve" if m % 2 == 0 else "act", Kk[:, pm, :n], ps[:, :n], [psb], [Kb])
            else:
                for tb in range(n // 128):
                    ps, psb = S.psum[nps % 6], S.psb[nps % 6]
                    nps += 1
                    for k in range(NCH):
                        mm(P, psb, ps[:, :], h[:, k, tb * 128:(tb + 1) * 128], w3[:, k, :], [wsb, hb[k]],
                           start=(k == 0), stop=(k == NCH - 1))
                    c0 = tb * 1024 + (i - 4) * 512
                    cpy(P, "dve" if tb % 2 == 0 else "act", V[:, c0:c0 + 512], ps[:, :], [psb], [Vb])
            ring.done(it)
            it += 1
        if s == 0:
            P.dma(q_d[:, T0:T0 + n].rearrange("(h p) t -> p h t", p=128), Q[:, :, :n], reads=[Qb])
        P.dma(k_d[:, T0:T0 + n].rearrange("(h p) t -> p h t", p=128), Kk[:, :, :n], reads=[Kb])
        P.dma(v_d[T0:T0 + n, :].rearrange("(tb p) c -> p tb c", p=128),
              V[:, :(n // 128) * 1024].rearrange("p (tb c) -> p tb c", c=1024), reads=[Vb])
    P.barrier()

    wa = Alloc(S.ar, OFF_WORK, ARENA_BYTES)
    NKC = T // 128
    NQ = L // 512
    st32 = wa.f32(512)
    stb = Buf()
    Kh = [(wa.bf(T), Buf()) for _ in range(2)]
    Vh = [(wa.bf(NKC * 128).rearrange("p (kc d) -> p kc d", d=128), Buf()) for _ in range(2)]
    W2 = [(wa.bf(23 * 64), Buf()) for _ in range(2)]
    qt = [(wa.bf(512), Buf()) for _ in range(NQ)]
    Pt = [(wa.bf(512), Buf()) for _ in range(4)]
    rl = wa.f32(512)
    rlb = Buf()
    ost = [(wa.bf(512), Buf()) for _ in range(2)]
    osbs = [(wa.f32(512), Buf()) for _ in range(2)]
    for i_ in range(2):
        mset(P, "pool", Vh[i_][0][:, :, 64:128], 1.0, [Vh[i_][1]])
        mset(P, "dve", Kh[i_][0][64:128, :], 0.0, [Kh[i_][1]])
    for i_ in range(NQ):
        mset(P, "pool", qt[i_][0][64:128, :], 0.0, [qt[i_][1]])
    for c0 in range(0, T, 512):
        wdt = min(512, T - c0)
        P.dma(st32[64:80, :wdt], I.na_koh[:, c0:c0 + wdt], writes=[stb])
        for i_ in range(2):
            cpy(P, "dve", Kh[i_][0][64:80, c0:c0 + wdt], st32[64:80, :wdt], [stb], [Kh[i_][1]])
    for i_ in range(NQ):
        P.dma(st32[64:80, :], I.na_bq[i_], writes=[stb])
        cpy(P, "dve", qt[i_][0][64:80, :], st32[64:80, :], [stb], [qt[i_][1]])

    def load_head(hd):
        P.dma(Kh[hd % 2][0][0:64, :], k_d[hd * 64:(hd + 1) * 64, :], writes=[Kh[hd % 2][1]])
        vv = v_d[:, hd * 64:(hd + 1) * 64].rearrange("(kc p) d -> p kc d", p=128)
        P.dma(Vh[hd % 2][0][:, 0:17, 0:64], vv[:, 0:17, :], writes=[Vh[hd % 2][1]])
        P.dma(Vh[hd % 2][0][:, 17:34, 0:64], vv[:, 17:34, :], writes=[Vh[hd % 2][1]])
        P.dma(W2[hd % 2][0], w.w2[hd], writes=[W2[hd % 2][1]])

    def load_q(hd, qi):
        T0 = LC + qi * 512
        P.dma(qt[qi][0][0:64, :], q_d[hd * 64:(hd + 1) * 64, T0:T0 + 512], writes=[qt[qi][1]])

    load_head(0)
    for qi in range(NQ):
        load_q(0, qi)
    nS = 0
    for hd in range(16):
        if hd + 1 < 16:
            load_head(hd + 1)
        Kt, Ktb = Kh[hd % 2]
        Vt, Vtb = Vh[hd % 2]
        W2t, W2b = W2[hd % 2]
        for qi in range(NQ):
            g = hd * NQ + qi
            q_, qb_ = qt[qi]
            r0 = qi * 8
            T0 = LC + qi * 512
            chunks = [(0, None), (1, None)] + [(2 + k0 // 2, k0) for k0 in na_chunks(r0)]
            ob = 4 + g % 2
            lb = 6 + g % 2
            pso, psob = S.psum[ob], S.psb[ob]
            psl, pslb = S.psum[lb], S.psb[lb]

            def issue_S(i):
                kc, k0 = chunks[i]
                bank = (nS + i) % 4
                ps, psb = S.psum[bank], S.psb[bank]
                mm(P, psb, ps[:, :], Kt[:, kc * 128:(kc + 1) * 128], q_[:, :], [Ktb, qb_],
                   start=True, stop=(k0 is None))
                if k0 is not None:
                    delta = k0 - r0
                    mm(P, psb, ps[:, :], S.ident, W2t[:, (10 - delta) * 64:(18 - delta) * 64], [S.cbuf, W2b],
                       start=False, stop=True)

            issue_S(0)
            issue_S(1)
            for i, (kc, k0) in enumerate(chunks):
                if i + 2 < len(chunks):
                    issue_S(i + 2)
                bank = (nS + i) % 4
                ps, psb = S.psum[bank], S.psb[bank]
                pt, ptb = Pt[(nS + i) % 4]
                actf(P, pt[:, :], ps[:, :], AF.Exp, [psb], [ptb])
                last = (i == len(chunks) - 1)
                mm(P, psob, pso[:, :], Vt[:, kc, :], pt[:, :], [Vtb, ptb], start=(i == 0), stop=last, inc=True)
            nS += len(chunks)
            if hd + 1 < 16:
                load_q(hd + 1, qi)
            osb, osbb = osbs[g % 2]
            cpy(P, "act", osb[:, :], pso[:, :], [psob], [osbb])
            mm(P, pslb, psl[0:64, :], S.ident32[:, 64:128], osb[:, :], [S.cbuf, osbb], start=True, stop=True)
            P.op("dve", lambda e, o=rl[0:64, :], i_=psl[0:64, :]: e.reciprocal(out=o, in_=i_), [pslb], [rlb])
            oo, oob = ost[g % 2]
            tt(P, "dve", oo[0:64, :], osb[0:64, :], rl[0:64, :], ALU.mult, [osbb, rlb], [oob])
            P.dma(o_d[hd * 64:(hd + 1) * 64, T0:T0 + 512], oo[0:64, :], reads=[oob])
    P.barrier()
    outproj_phase(S, l, o_d.rearrange("(h p) t -> p h t", p=128), w.wo, attn_tiles(False, 512))


def _fm(v, nchunk):
    v = np.asarray(v, np.float32)
    lead = v.shape[:-1]
    r = v.reshape(lead + (nchunk, 128))
    r = np.moveaxis(r, -1, 0)
    return np.ascontiguousarray(r.reshape(128, -1))


def host_prep(inp):
    f = lambda a: np.ascontiguousarray(np.asarray(a, np.float32))
    x, c, ctx, c_ctx = f(inp["x"]), f(inp["c"]), f(inp["ctx"]), f(inp["c_ctx"])
    shared = {}
    shared["ada_w"] = f(inp["ada_w"])
    shared["ada_bT"] = _fm(inp["ada_b"], 48)
    shared["norm_gT"] = _fm(inp["norm_g"], NCH)
    shared["ffn_w_gu"] = f(inp["ffn_w_gu"])
    shared["ffn_w_down"] = f(inp["ffn_w_down"])
    shared["pool_w"] = f(inp["pool_w"])
    shared["pool_scaleT"] = _fm(inp["pool_scale"], NCH)
    corr = np.ones((4, 2, 8), np.float32)
    for g, w in enumerate((2, 4, 8, 16)):
        for t in range(w // 2):
            corr[g, 0, t] = np.float32(w) / np.float32(t + w // 2)
        for i in range(8):
            d = 8 - i
            if d <= w // 2 - 1:
                corr[g, 1, i] = np.float32(w) / np.float32(d + w // 2)
    shared["pool_corr"] = np.ascontiguousarray(np.broadcast_to(corr.reshape(1, 64), (128, 64)))
    w_in = f(inp["mla_w_in"])[0]
    kr = w_in[:, 640:704]
    shared["mla_w_in"] = np.ascontiguousarray(np.concatenate([w_in, kr[:, 32:], kr[:, :32]], axis=1))
    wqb = f(inp["mla_w_qb"])[0].reshape(384, 8, 192)
    qr = wqb[:, :, 128:]
    shared["mla_w_qb"] = np.ascontiguousarray(np.concatenate(
        [wqb[:, :, :128].reshape(384, 1024), qr.reshape(384, 512),
         np.concatenate([qr[:, :, 32:], qr[:, :, :32]], axis=2).reshape(384, 512)], axis=1))
    wkvb = f(inp["mla_w_kvb"])[0].reshape(256, 8, 256)
    shared["mla_w_kvb"] = np.ascontiguousarray(np.concatenate(
        [wkvb[:, :, :128].reshape(256, 1024), wkvb[:, :, 128:].reshape(256, 1024)], axis=1))
    shared["mla_nT"] = np.ascontiguousarray(np.concatenate(
        [_fm(inp["mla_q_norm"][0], 3), _fm(inp["mla_kv_norm"][0], 2)], axis=1))
    shared["mla_w_o"] = f(inp["mla_w_o"])[0]
    t = np.arange(L)
    row = (t // GRID).astype(np.float32)
    col = (t % GRID).astype(np.float32)
    inv = (np.float32(10000.0) ** (-np.arange(0, 32, 2, dtype=np.float32) / np.float32(32))).astype(np.float32)
    ang = np.concatenate([row[:, None] * inv, col[:, None] * inv], axis=-1).astype(np.float32)
    cos, sin = np.cos(ang).T.astype(np.float32), np.sin(ang).T.astype(np.float32)
    shared["rope_cs"] = np.ascontiguousarray(np.stack(
        [np.concatenate([cos, cos], 0), np.concatenate([-sin, sin], 0)], 0))
    shared["na_w_in"] = f(inp["na_w_in"])[0]
    shared["na_w_o"] = f(inp["na_w_o"])[0]
    shared.update(na_tables(f(inp["na_rpb"])[0]))
    shared["ident"] = np.eye(128, dtype=np.float32)
    maps = []
    for b in range(x.shape[0]):
        m = dict(shared)
        m["xT"] = np.ascontiguousarray(x[b].T)
        m["ctxT"] = np.ascontiguousarray(ctx[b].T)
        m["cc"] = _fm(np.stack([c[b], c_ctx], 0), NCH).reshape(128, 2, NCH).transpose(0, 2, 1).reshape(128, 16).copy()
        maps.append(m)
    return maps


def na_row_start(r):
    return min(max(r - 4, 0), GRID - 8)


def na_chunks(r0):
    lo = na_row_start(r0)
    hi = na_row_start(r0 + 7) + 8
    return list(range(lo - lo % 2, hi, 2))


def na_tile_types():
    types = {}
    for r0 in (0, 8, 56):
        for k0 in na_chunks(r0):
            key = ("first", k0) if r0 == 0 else (("last", k0) if r0 == 56 else ("mid", k0 - r0))
            types.setdefault(key, len(types))
    return types


def na_type_of(r0, k0, types):
    if r0 == 0:
        return types[("first", k0)]
    if r0 == 56:
        return types[("last", k0)]
    return types[("mid", k0 - r0)]


def na_tables(rpb):
    kc = np.arange(64)[:, None]
    qc = np.arange(64)[None, :]
    cs = np.clip(qc - 8, 0, 48)
    cmask = (kc >= cs) & (kc < cs + 16)
    dc = np.clip(kc - qc + 15, 0, 30)
    w2 = np.zeros((16, 128, 23, 64), np.float32)
    for a in range(2):
        for e in range(23):
            dr = 10 + a - e
            dri = int(np.clip(dr + 7, 0, 14))
            vals = rpb[:, dri][:, dc]
            w2[:, a * 64:(a + 1) * 64, e, :] = np.where(cmask[None], vals, np.float32(NEG))
    koh = np.zeros((16, T), np.float32)
    for krow in range(GRID):
        koh[krow % 16, LC + krow * 64: LC + (krow + 1) * 64] = 1.0
    bq = np.zeros((8, 16, 512), np.float32)
    for qi in range(8):
        r0 = qi * 8
        lo = na_chunks(r0)[0]
        for krow in range(lo, min(lo + 16, GRID)):
            for b in range(8):
                rs = na_row_start(r0 + b)
                ok = rs <= krow < rs + 8
                bq[qi, krow % 16, b * 64:(b + 1) * 64] = 0.0 if ok else NEG
    return {"na_w2": np.ascontiguousarray(w2.reshape(16, 128, 23 * 64)), "na_koh": koh, "na_bq": bq}


_NC_CACHE = {}


def kernel(**inputs):
    maps = host_prep(inputs)
    key = "full"
    if key not in _NC_CACHE:
        _NC_CACHE[key] = build([0, 1, 2, 3])
    nc = _NC_CACHE[key]
    res = run_bass_kernel_spmd(nc, maps, core_ids=list(range(8)))
    out = np.stack([np.ascontiguousarray(r["yT"].T) for r in res.results], 0)
    return out.astype(np.float32)
```
